# Optimizing a Trainium2 kernel written in Bass

```python
import math
import jax, jax.numpy as jnp
from jax import lax
import numpy as np


D_MODEL = 4096
BATCH = 16
SEQ = 256
DEPTH = 4
DEC_BATCH = 8
DEC_SEQ = 4096
PAST_LEN = 256

GRID_W = 64
HEAD_DIM = 128
Q_BLOCK = 128
N_MIXERS = 4
H_A = 16
KVH_A = 4
H_B = 16
KVH_B = 4
WINDOW = 128
H_C = 8
KVH_C = 2
H_D = 16
KVH_D = 16
NA_ROWS = 8
NA_COLS = 16
D_FF = 2048
ROPE_THETA = 10000.0
EPS = 1e-6
N_MOD = 9
NEG_INF = -1e30
F32 = jnp.float32

kernel_name = 'hybrid_diffusion_prefix_trunk_step'


def rms_norm(x, g):
    xf = x.astype(F32)
    y = xf * lax.rsqrt(jnp.mean(xf * xf, axis=-1, keepdims=True) + EPS)
    return (y * g.astype(F32)).astype(x.dtype)


def modulate(h, shift, scale):
    return h * (1.0 + scale[:, None, :]) + shift[:, None, :]


def swiglu(h, w_in, w_out):
    gate, up = jnp.split(h @ w_in, 2, axis=-1)
    return (jax.nn.silu(gate) * up) @ w_out


def half_ffn(x, shift, scale, gate, g, w_in, w_out):
    h = modulate(rms_norm(x, g), shift, scale)
    return x + 0.5 * gate[:, None, :] * swiglu(h, w_in, w_out)


def axial_rope(S):
    t = jnp.arange(S)
    row = (t // GRID_W).astype(F32)
    col = (t % GRID_W).astype(F32)
    nf = HEAD_DIM // 4
    inv = ROPE_THETA ** (-jnp.arange(nf, dtype=F32) / nf)
    ang = jnp.concatenate([row[:, None] * inv, col[:, None] * inv], axis=-1)
    return jnp.cos(ang), jnp.sin(ang)


def apply_rope(x, cos, sin):
    shp = (1, cos.shape[0]) + (1,) * (x.ndim - 3) + (cos.shape[1],)
    cs = cos.reshape(shp).astype(x.dtype)
    sn = sin.reshape(shp).astype(x.dtype)
    x1, x2 = jnp.split(x, 2, axis=-1)
    return jnp.concatenate([x1 * cs - x2 * sn, x2 * cs + x1 * sn], axis=-1)


def sweep_query_blocks(fn, q):
    B, S = q.shape[0], q.shape[1]
    nb = S // Q_BLOCK
    qb = jnp.moveaxis(q.reshape((B, nb, Q_BLOCK) + q.shape[2:]), 1, 0)
    ob = lax.map(lambda a: fn(a[0], a[1]), (jnp.arange(nb), qb))
    return jnp.moveaxis(ob, 0, 1).reshape((B, S) + ob.shape[3:])


def attend(q, k, v, mask=None, bias=None, sink=None):
    s = jnp.einsum('bqhgd,bkhd->bhgqk', q.astype(F32), k.astype(F32)) * (q.shape[-1] ** -0.5)
    if bias is not None:
        s = s + bias
    if mask is not None:
        s = jnp.where(mask, s, NEG_INF)
    m = jnp.max(s, axis=-1, keepdims=True)
    if sink is not None:
        sk = sink.astype(F32)[None, :, :, None, None]
        m = jnp.maximum(m, sk)
    e = jnp.exp(s - m)
    l = jnp.sum(e, axis=-1, keepdims=True)
    if sink is not None:
        l = l + jnp.exp(sk - m)
    l_t = jnp.transpose(l[..., 0], (0, 3, 1, 2))[..., None]
    o = jnp.einsum('bhgqk,bkhd->bqhgd', e, v.astype(F32)) / l_t
    return o.astype(v.dtype)


def gqa_proj(h, w_qkv, qn, kn, n_heads, n_kv):
    B, S, _ = h.shape
    q, k, v = jnp.split(h @ w_qkv, [n_heads * HEAD_DIM, (n_heads + n_kv) * HEAD_DIM], axis=-1)
    q = rms_norm(q.reshape(B, S, n_kv, n_heads // n_kv, HEAD_DIM), qn)
    k = rms_norm(k.reshape(B, S, n_kv, HEAD_DIM), kn)
    return q, k, v.reshape(B, S, n_kv, HEAD_DIM)


def context_gqa(h, w_qkv, w_o, qn, kn, n_heads, n_kv, sink):
    B, S, _ = h.shape
    q, k, v = gqa_proj(h, w_qkv, qn, kn, n_heads, n_kv)
    o = sweep_query_blocks(lambda j, qb: attend(qb, k, v, sink=sink), q)
    return o.reshape(B, S, -1) @ w_o, k, v


def latent_axial_gqa(h, k_ctx, v_ctx, w_qkv, w_o, qn, kn):
    B, S, _ = h.shape
    q, k, v = gqa_proj(h, w_qkv, qn, kn, H_A, KVH_A)
    cos, sin = axial_rope(S)
    q = apply_rope(q, cos, sin)
    k = apply_rope(k, cos, sin)
    k_all = jnp.concatenate([k, k_ctx.astype(k.dtype)], axis=1)
    v_all = jnp.concatenate([v, v_ctx.astype(v.dtype)], axis=1)
    o = sweep_query_blocks(lambda j, qb: attend(qb, k_all, v_all), q)
    return o.reshape(B, S, -1) @ w_o


def latent_window_gqa(h, k_ctx, v_ctx, w_qkv, w_o, qn, kn, sink):
    B, S, _ = h.shape
    L = k_ctx.shape[1]
    q, k, v = gqa_proj(h, w_qkv, qn, kn, H_B, KVH_B)
    cos, sin = axial_rope(S)
    q = apply_rope(q, cos, sin)
    k = apply_rope(k, cos, sin)
    band = Q_BLOCK + 2 * WINDOW
    pad = ((0, 0), (WINDOW, WINDOW), (0, 0), (0, 0))
    kp = jnp.pad(k, pad)
    vp = jnp.pad(v, pad)
    k_ctx = k_ctx.astype(k.dtype)
    v_ctx = v_ctx.astype(v.dtype)
    ctx_ok = jnp.ones((Q_BLOCK, L), dtype=bool)

    def block(j, qb):
        s0 = j * Q_BLOCK
        kb = lax.dynamic_slice_in_dim(kp, s0, band, axis=1)
        vb = lax.dynamic_slice_in_dim(vp, s0, band, axis=1)
        qpos = s0 + jnp.arange(Q_BLOCK)
        kpos = s0 - WINDOW + jnp.arange(band)
        ok = (kpos[None, :] >= 0) & (kpos[None, :] < S) & (jnp.abs(qpos[:, None] - kpos[None, :]) <= WINDOW)
        mask = jnp.concatenate([ok, ctx_ok], axis=1)
        return attend(qb, jnp.concatenate([kb, k_ctx], axis=1), jnp.concatenate([vb, v_ctx], axis=1), mask=mask, sink=sink)

    o = sweep_query_blocks(block, q)
    return o.reshape(B, S, -1) @ w_o


def diff_proj(h, w_qkv, qn, kn):
    B, S, _ = h.shape
    nq = H_C * 2 * HEAD_DIM
    nk = KVH_C * 2 * HEAD_DIM
    q, k, v = jnp.split(h @ w_qkv, [nq, nq + nk], axis=-1)
    q = rms_norm(q.reshape(B, S, KVH_C, H_C // KVH_C, 2, HEAD_DIM), qn)
    k = rms_norm(k.reshape(B, S, KVH_C, 2, HEAD_DIM), kn)
    return q, k, v.reshape(B, S, KVH_C, 2 * HEAD_DIM)


def diff_lambda(lq1, lk1, lq2, lk2, lam_init):
    e1 = jnp.exp(jnp.sum(lq1.astype(F32) * lk1.astype(F32)))
    e2 = jnp.exp(jnp.sum(lq2.astype(F32) * lk2.astype(F32)))
    return e1 - e2 + lam_init


def diff_attend(q, k, v, lam):
    s = jnp.einsum('bqhgid,bkhid->ibhgqk', q.astype(F32), k.astype(F32)) * (HEAD_DIM ** -0.5)
    e = jnp.exp(s - jnp.max(s, axis=-1, keepdims=True))
    l = jnp.sum(e, axis=-1)
    l_t = jnp.transpose(l, (0, 1, 4, 2, 3))[..., None]
    o = jnp.einsum('ibhgqk,bkhd->ibqhgd', e, v.astype(F32)) / l_t
    return (o[0] - lam * o[1]).astype(v.dtype)


def diff_out(o, subln, lam_init, w_o):
    B, S = o.shape[0], o.shape[1]
    o = rms_norm(o, subln) * (1.0 - lam_init)
    return o.reshape(B, S, -1) @ w_o


def context_diff(h, w_qkv, w_o, qn, kn, lam, lam_init, subln):
    q, k, v = diff_proj(h, w_qkv, qn, kn)
    o = sweep_query_blocks(lambda j, qb: diff_attend(qb, k, v, lam), q)
    return diff_out(o, subln, lam_init, w_o), k, v


def latent_diff(h, k_ctx, v_ctx, w_qkv, w_o, qn, kn, lam, lam_init, subln):
    S = h.shape[1]
    q, k, v = diff_proj(h, w_qkv, qn, kn)
    cos, sin = axial_rope(S)
    q = apply_rope(q, cos, sin)
    k = apply_rope(k, cos, sin)
    k_all = jnp.concatenate([k, k_ctx.astype(k.dtype)], axis=1)
    v_all = jnp.concatenate([v, v_ctx.astype(v.dtype)], axis=1)
    o = sweep_query_blocks(lambda j, qb: diff_attend(qb, k_all, v_all, lam), q)
    return diff_out(o, subln, lam_init, w_o)


def latent_natten(h, k_ctx, v_ctx, w_qkv, w_o, qn, kn, rpb):
    B, S, _ = h.shape
    L = k_ctx.shape[1]
    q, k, v = gqa_proj(h, w_qkv, qn, kn, H_D, KVH_D)
    rows = S // GRID_W
    kr_n = min(NA_ROWS, rows)
    q_rows = Q_BLOCK // GRID_W
    span = min(kr_n + q_rows - 1, rows)
    kg = k.reshape(B, rows, GRID_W, KVH_D, HEAD_DIM)
    vg = v.reshape(B, rows, GRID_W, KVH_D, HEAD_DIM)
    k_ctx = k_ctx.astype(k.dtype)
    v_ctx = v_ctx.astype(v.dtype)
    ctx_ok = jnp.ones((Q_BLOCK, L), dtype=bool)
    ctx_bias = jnp.zeros((H_D, Q_BLOCK, L), F32)
    rpb_f = rpb.astype(F32)
    kidx = jnp.arange(span * GRID_W)

    def block(j, qb):
        t = j * Q_BLOCK + jnp.arange(Q_BLOCK)
        r = t // GRID_W
        cl = t % GRID_W
        rs = jnp.clip(r - kr_n // 2, 0, rows - kr_n)
        cs = jnp.clip(cl - NA_COLS // 2, 0, GRID_W - NA_COLS)
        start = jnp.minimum(jnp.clip(j * q_rows - kr_n // 2, 0, rows - kr_n), rows - span)
        kb = lax.dynamic_slice_in_dim(kg, start, span, axis=1).reshape(B, span * GRID_W, KVH_D, HEAD_DIM)
        vb = lax.dynamic_slice_in_dim(vg, start, span, axis=1).reshape(B, span * GRID_W, KVH_D, HEAD_DIM)
        kr = start + kidx // GRID_W
        kc = kidx % GRID_W
        ok = ((kr[None, :] >= rs[:, None]) & (kr[None, :] < rs[:, None] + kr_n)
              & (kc[None, :] >= cs[:, None]) & (kc[None, :] < cs[:, None] + NA_COLS))
        dr = jnp.clip(kr[None, :] - r[:, None] + NA_ROWS - 1, 0, 2 * NA_ROWS - 2)
        dc = jnp.clip(kc[None, :] - cl[:, None] + NA_COLS - 1, 0, 2 * NA_COLS - 2)
        bias = jnp.concatenate([rpb_f[:, dr, dc], ctx_bias], axis=-1)
        bias = bias.reshape(KVH_D, H_D // KVH_D, Q_BLOCK, -1)
        mask = jnp.concatenate([ok, ctx_ok], axis=1)
        return attend(qb, jnp.concatenate([kb, k_ctx], axis=1), jnp.concatenate([vb, v_ctx], axis=1), mask=mask, bias=bias)

    o = sweep_query_blocks(block, q)
    return o.reshape(B, S, -1) @ w_o


def setup_inputs(seed: int = 0) -> dict:
    key = jax.random.key(seed)
    keys = iter(jax.random.split(key, 48))

    def nrm(shape, scale=1.0):
        return jax.random.normal(next(keys), shape, jnp.float32) * scale

    def gain(shape):
        return 1.0 + nrm(shape, 0.02)

    D = D_MODEL
    HD = HEAD_DIM
    inp = {}
    inp['x_prompt'] = nrm((BATCH, SEQ, D))
    inp['x_sample'] = nrm((DEC_BATCH, DEC_SEQ, D))
    inp['cache_k0'] = nrm((DEC_BATCH, PAST_LEN, KVH_A, HD))
    inp['cache_v0'] = nrm((DEC_BATCH, PAST_LEN, KVH_A, HD))
    inp['cache_k1'] = nrm((DEC_BATCH, PAST_LEN, KVH_B, HD))
    inp['cache_v1'] = nrm((DEC_BATCH, PAST_LEN, KVH_B, HD))
    inp['cache_k2'] = nrm((DEC_BATCH, PAST_LEN, KVH_C, 2, HD))
    inp['cache_v2'] = nrm((DEC_BATCH, PAST_LEN, KVH_C, 2 * HD))
    inp['cache_k3'] = nrm((DEC_BATCH, PAST_LEN, KVH_D, HD))
    inp['cache_v3'] = nrm((DEC_BATCH, PAST_LEN, KVH_D, HD))
    inp['c'] = nrm((DEC_BATCH, D))
    inp['c_ctx'] = nrm((D,))
    inp['norm_g'] = gain((DEPTH, 3, D))
    inp['w_ada'] = nrm((DEPTH, D, N_MOD * D), 0.5 * D ** -0.5)
    inp['b_ada'] = nrm((DEPTH, N_MOD * D), 0.02)
    inp['w_ffn_in'] = nrm((DEPTH, 2, D, 2 * D_FF), D ** -0.5)
    inp['w_ffn_out'] = nrm((DEPTH, 2, D_FF, D), D_FF ** -0.5)
    inp['att_wqkv'] = nrm((D, (H_A + 2 * KVH_A) * HD), D ** -0.5)
    inp['att_wo'] = nrm((H_A * HD, D), (H_A * HD) ** -0.5)
    inp['att_qn'] = gain((HD,))
    inp['att_kn'] = gain((HD,))
    inp['win_wqkv'] = nrm((D, (H_B + 2 * KVH_B) * HD), D ** -0.5)
    inp['win_wo'] = nrm((H_B * HD, D), (H_B * HD) ** -0.5)
    inp['win_qn'] = gain((HD,))
    inp['win_kn'] = gain((HD,))
    inp['win_sink'] = nrm((H_B,), 0.5)
    inp['diff_wqkv'] = nrm((D, (2 * H_C + 4 * KVH_C) * HD), D ** -0.5)
    inp['diff_wo'] = nrm((2 * H_C * HD, D), (2 * H_C * HD) ** -0.5)
    inp['diff_qn'] = gain((HD,))
    inp['diff_kn'] = gain((HD,))
    inp['diff_lq1'] = nrm((HD,), 0.1)
    inp['diff_lk1'] = nrm((HD,), 0.1)
    inp['diff_lq2'] = nrm((HD,), 0.1)
    inp['diff_lk2'] = nrm((HD,), 0.1)
    inp['diff_subln'] = gain((2 * HD,))
    inp['nat_wqkv'] = nrm((D, (H_D + 2 * KVH_D) * HD), D ** -0.5)
    inp['nat_wo'] = nrm((H_D * HD, D), (H_D * HD) ** -0.5)
    inp['nat_qn'] = gain((HD,))
    inp['nat_kn'] = gain((HD,))
    inp['nat_rpb'] = nrm((H_D, 2 * NA_ROWS - 1, 2 * NA_COLS - 1), 0.1)
    return inp


def reference(x_prompt, x_sample, cache_k0, cache_v0, cache_k1, cache_v1, cache_k2, cache_v2,
              cache_k3, cache_v3, c, c_ctx, norm_g, w_ada, b_ada, w_ffn_in, w_ffn_out,
              att_wqkv, att_wo, att_qn, att_kn,
              win_wqkv, win_wo, win_qn, win_kn, win_sink,
              diff_wqkv, diff_wo, diff_qn, diff_kn, diff_lq1, diff_lk1, diff_lq2, diff_lk2, diff_subln,
              nat_wqkv, nat_wo, nat_qn, nat_kn, nat_rpb):
    caches = [(cache_k0, cache_v0), (cache_k1, cache_v1), (cache_k2, cache_v2), (cache_k3, cache_v3)]
    sink = win_sink.reshape(KVH_B, H_B // KVH_B)
    xp, xs = x_prompt, x_sample
    new_state = []
    for i in range(DEPTH):
        kind = i % N_MIXERS
        mp = jnp.split(jax.nn.silu(c_ctx)[None, :] @ w_ada[i] + b_ada[i], N_MOD, axis=-1)
        ms = jnp.split(jax.nn.silu(c) @ w_ada[i] + b_ada[i], N_MOD, axis=-1)
        xp = half_ffn(xp, mp[0], mp[1], mp[2], norm_g[i, 0], w_ffn_in[i, 0], w_ffn_out[i, 0])
        xs = half_ffn(xs, ms[0], ms[1], ms[2], norm_g[i, 0], w_ffn_in[i, 0], w_ffn_out[i, 0])
        hp = modulate(rms_norm(xp, norm_g[i, 1]), mp[3], mp[4])
        hs = modulate(rms_norm(xs, norm_g[i, 1]), ms[3], ms[4])
        k_cache, v_cache = caches[i]
        if kind == 0:
            yp, kp, vp = context_gqa(hp, att_wqkv, att_wo, att_qn, att_kn, H_A, KVH_A, None)
            ys = latent_axial_gqa(hs, k_cache, v_cache, att_wqkv, att_wo, att_qn, att_kn)
        elif kind == 1:
            yp, kp, vp = context_gqa(hp, win_wqkv, win_wo, win_qn, win_kn, H_B, KVH_B, sink)
            ys = latent_window_gqa(hs, k_cache, v_cache, win_wqkv, win_wo, win_qn, win_kn, sink)
        elif kind == 2:
            lam_init = 0.8 - 0.6 * math.exp(-0.3 * i)
            lam = diff_lambda(diff_lq1, diff_lk1, diff_lq2, diff_lk2, lam_init)
            yp, kp, vp = context_diff(hp, diff_wqkv, diff_wo, diff_qn, diff_kn, lam, lam_init, diff_subln)
            ys = latent_diff(hs, k_cache, v_cache, diff_wqkv, diff_wo, diff_qn, diff_kn, lam, lam_init, diff_subln)
        else:
            yp, kp, vp = context_gqa(hp, nat_wqkv, nat_wo, nat_qn, nat_kn, H_D, KVH_D, None)
            ys = latent_natten(hs, k_cache, v_cache, nat_wqkv, nat_wo, nat_qn, nat_kn, nat_rpb)
        xp = xp + mp[5][:, None, :] * yp
        xs = xs + ms[5][:, None, :] * ys
        xp = half_ffn(xp, mp[6], mp[7], mp[8], norm_g[i, 2], w_ffn_in[i, 1], w_ffn_out[i, 1])
        xs = half_ffn(xs, ms[6], ms[7], ms[8], norm_g[i, 2], w_ffn_in[i, 1], w_ffn_out[i, 1])
        new_state.append(kp)
        new_state.append(vp)
    return (xp, xs, new_state[0], new_state[1], new_state[2], new_state[3],
            new_state[4], new_state[5], new_state[6], new_state[7])
```

```python
import contextlib
import math
import numpy as np
import concourse.bass as bass
import concourse.mybir as mybir
from concourse.bass_utils import run_bass_kernel_spmd

F32 = mybir.dt.float32
BF16 = mybir.dt.bfloat16
AF = mybir.ActivationFunctionType
ALU = mybir.AluOpType

D = 4096
NKC = 32
TW = 512
NS = 4096
NCTX = 256
NPR = 512
NTOK = NS + NPR
TSP = NS + NCTX + NPR
PR0 = NS + NCTX
NT = 9
EPS = 1e-6
HD = 128
SCALE = HD ** -0.5
NCH = [68, 68, 68, 80]
GEO = [(16, 4, 512), (16, 4, 512), (16, 4, 512), (16, 16, 2048)]


class Buf:
    __slots__ = ("w", "r", "x")

    def __init__(self, x=False):
        self.w = None
        self.r = []
        self.x = x


class Op:
    __slots__ = ("eng", "fn", "deps", "kind", "hasdep", "sem", "val")

    def __init__(self, eng, fn, kind):
        self.eng = eng
        self.fn = fn
        self.kind = kind
        self.deps = ()
        self.hasdep = False
        self.sem = None
        self.val = 0


class Prog:
    ENGS = ("pe", "act", "dve", "pool", "sp")
    DQ = ("sp", "pool")

    def __init__(self, nc):
        self.nc = nc
        self.q = {e: [] for e in self.ENGS}
        self.epoch = 0
        self.bar = {}
        self.dmas = []

    def op(self, eng, name, kw, reads=(), writes=(), kind="c"):
        o = Op(eng, (name, kw), kind)
        xr = [b for b in reads if b.x]
        if xr:
            reads = [b for b in reads if not b.x]
            writes = list(writes) + xr
        deps = set()
        for b in reads:
            if b.w is not None:
                deps.add(b.w)
        for b in writes:
            if b.w is not None:
                deps.add(b.w)
            deps.update(b.r)
        if self.bar.get(eng):
            deps.update(self.bar.pop(eng))
        if eng == "pe" and kind == "c":
            deps = {d for d in deps if not (d.eng == "pe" and d.kind == "c")}
        o.deps = deps
        for d in deps:
            d.hasdep = True
        for b in writes:
            b.w = o
            b.r = []
        for b in reads:
            if b.w is not o:
                b.r.append(o)
        self.q[eng].append((o, self.epoch))
        if kind != "c":
            self.dmas.append(o)
        return o

    def barrier(self):
        lasts = [self.q[e][-1][0] for e in ("pe", "act", "dve", "pool") if self.q[e]]
        lasts += self.dmas
        self.dmas = []
        for e in self.ENGS:
            self.bar[e] = list(lasts)
        self.epoch += 1

    def emit(self, stack):
        nc = self.nc
        nep = self.epoch + 1
        csem = {}
        for e in ("pe", "act", "dve", "pool"):
            for ep in range(nep):
                csem[(e, ep)] = stack.enter_context(nc.semaphore(f"c_{e}_{ep}"))
        K = 8
        dsem = {e: [stack.enter_context(nc.semaphore(f"d_{e}_{i}")) for i in range(K)] for e in self.DQ}
        cnt = {k: 0 for k in csem}
        dcnt = {e: 0 for e in dsem}
        dval = {e: [0] * K for e in dsem}
        ncc = 0
        plan = {e: [] for e in self.ENGS}
        for e in self.ENGS:
            for (o, ep) in self.q[e]:
                pre = None
                if o.kind == "c":
                    if o.hasdep:
                        cnt[(e, ep)] += 1
                        o.sem = csem[(e, ep)]
                        o.val = cnt[(e, ep)]
                elif o.kind == "d":
                    i = dcnt[e] % K
                    dcnt[e] += 1
                    prev = dval[e][i]
                    if prev > 0:
                        pre = (dsem[e][i], prev)
                    dval[e][i] = prev + 16
                    o.sem = dsem[e][i]
                    o.val = prev + 16
                else:
                    o.sem = stack.enter_context(nc.semaphore(f"cc_{ncc}"))
                    ncc += 1
                    o.val = 1
                plan[e].append((o, pre))
        assert max(cnt.values()) < 30000, max(cnt.values())
        block = stack.enter_context(nc.Block())
        handles = {"pe": "tensor", "act": "scalar", "dve": "vector", "pool": "gpsimd", "sp": "sync"}

        def mk(e):
            def body(eng):
                waited = {}
                for (o, pre) in plan[e]:
                    need = {}
                    if pre is not None:
                        need[id(pre[0])] = pre
                    for d in o.deps:
                        k = id(d.sem)
                        if k not in need or need[k][1] < d.val:
                            need[k] = (d.sem, d.val)
                    for k, (s, v) in need.items():
                        if waited.get(k, 0) < v:
                            eng.wait_ge(s, v)
                            waited[k] = v
                    ins = getattr(eng, o.fn[0])(**o.fn[1])
                    if o.kind == "c":
                        if o.hasdep:
                            ins.then_inc(o.sem, 1)
                    elif o.kind == "d":
                        ins.then_inc(o.sem, 16)
                    else:
                        ins.then_inc(o.sem, 1)
                if e in dsem:
                    for i in range(K):
                        if dval[e][i] > 0 and waited.get(id(dsem[e][i]), 0) < dval[e][i]:
                            eng.wait_ge(dsem[e][i], dval[e][i])
            return body

        for e in self.ENGS:
            getattr(block, handles[e])(mk(e))


def build_nc(stop=None, tiles=None):
    TILES = list(range(NT)) if tiles is None else tiles
    nc = bass.Bass("TRN2", target_bir_lowering=False)
    P = Prog(nc)

    def din(name, shape, dt=F32):
        return nc.dram_tensor(name, shape, dt, kind="ExternalInput")

    def dout(name, shape, dt=F32):
        return nc.dram_tensor(name, shape, dt, kind="ExternalOutput")

    xT = din("xT", [D, NTOK])
    wl = [din(f"wl{l}", [NCH[l] * 16, 8192]) for l in range(4)]
    wada = din("wada", [4 * 36 * 128, 4096])
    bada = din("bada", [128, 144])
    ccT = din("ccT", [128, 288])
    sel = din("sel", [128, 8])
    normg = din("normg", [128, 384])
    qkn = din("qkn", [128, 8])
    sinkb = din("sinkb", [128, 16])
    lvec = din("lvec", [128, 4])
    subln = din("subln", [128, 2])
    cosT = din("cosT", [128, NS])
    sinT = din("sinT", [128, NS])
    cst = din("cst", [128, 128 + 2 * 512])
    ckT = [din(f"ckT{l}", [GEO[l][1] * 128, NCTX]) for l in range(4)]
    cv = [din(f"cv{l}", [NCTX, GEO[l][2]]) for l in range(4)]
    natb = din("natb", [2 * 5 * 128, 640])
    natm = din("natm", [5 * 128, 640])

    yT = dout("yT", [D, NTOK])
    nk = [dout(f"nk{l}", [GEO[l][1] * 128, NPR]) for l in range(4)]
    nv = [dout(f"nv{l}", [NPR, GEO[l][2]]) for l in range(4)]

    wloc = [nc.dram_tensor(f"wloc{l}", [NCH[l] * 16, 8192], BF16) for l in range(4)]
    wfull = [nc.dram_tensor(f"wfull{l}", [NCH[l] * 128, 8192], BF16) for l in range(4)]
    modloc = nc.dram_tensor("modloc", [128, 1296], F32)
    modall = nc.dram_tensor("modall", [8 * 128, 1296], F32)
    eloc = nc.dram_tensor("eloc", [2 * 5 * 128, 640], BF16)
    etab = nc.dram_tensor("etab", [16 * 5 * 128, 640], BF16)
    QT = nc.dram_tensor("QT", [16 * 128, TSP], BF16)
    KT = nc.dram_tensor("KT", [16 * 128, TSP], BF16)
    VS = nc.dram_tensor("VS", [TSP, 2048], BF16)
    OT = nc.dram_tensor("OT", [16 * 128, TSP], BF16)

    with contextlib.ExitStack() as st:
        ARENA = 188 * 1024
        arena = st.enter_context(nc.sbuf_tensor("arena", [128, ARENA // 2], BF16))
        banks = [st.enter_context(nc.psum_tensor(f"bank{i}", [128, 512], F32)) for i in range(8)]
        bankb = [Buf(x=True) for _ in range(8)]
        bstate = {"i": 0}

        def next_bank():
            i = bstate["i"] % 8
            bstate["i"] += 1
            return banks[i], bankb[i]

        def view(off, shape, dt):
            n = 1
            for s in shape[1:]:
                n *= s
            esz = 4 if dt == F32 else 2
            assert off % 4 == 0
            a = arena[:, off // 2: off // 2 + n * esz // 2]
            if dt == F32:
                a = a.bitcast(F32)
            if len(shape) == 3:
                a = a.rearrange("p (a b) -> p a b", a=shape[1])
            elif len(shape) == 4:
                a = a.rearrange("p (a b c) -> p a b c", a=shape[1], b=shape[2])
            return a

        off = {"v": 0}

        def alloc(shape, dt):
            n = 1
            for s in shape[1:]:
                n *= s
            nb = n * (4 if dt == F32 else 2)
            nb = (nb + 31) // 32 * 32
            o = off["v"]
            off["v"] += nb
            assert off["v"] <= ARENA, off["v"]
            return view(o, shape, dt)

        ones = alloc([128, 128], BF16)
        perm = alloc([128, 128], BF16)
        wmask = alloc([128, 2, 512], BF16)
        epsT = alloc([128, 1], F32)
        mA = {g: alloc([128, 4, 3, 32], F32) for g in "SP"}
        mB = {g: alloc([128, 4, 3, 32], F32) for g in "SP"}
        mG = {g: alloc([128, 4, 3, 32], F32) for g in "SP"}
        qknT = alloc([128, 8], F32)
        sinkE = alloc([128, 16], F32)
        lamT = alloc([128, 8], F32)
        sublnS = alloc([128, 2], F32)
        cosS = alloc([128, 512], F32)
        sinS = alloc([128, 512], F32)
        sq = [alloc([128, 512], BF16) for _ in range(2)]
        rstd = alloc([128, 512], F32)
        tA = [alloc([128, 512], F32) for _ in range(2)]
        tS = [alloc([128, 512], F32) for _ in range(2)]
        Bsq = [Buf(), Buf()]
        Brstd = Buf()
        BtA = [Buf(), Buf()]
        BtS = [Buf(), Buf()]
        Bcs = Buf()
        wring_off = off["v"]
        wring = [alloc([128, 8192], BF16) for _ in range(3)]
        wringf = [view(wring_off + i * 16384, [128, 4096], F32) for i in range(3)]
        Bw = [Buf() for _ in range(3)]
        PH0 = off["v"]
        PHSIZE = ARENA - PH0
        assert PHSIZE >= 112 * 1024, PHSIZE

        def phase_alloc():
            off["v"] = PH0

        SP_ = "sp"

        def dma(out, in_, reads=(), writes=(), q=SP_):
            return P.op(q, "dma_start", dict(out=out, in_=in_), reads=reads, writes=writes, kind="d")

        stages = []

        def run_stages():
            widx = [i for i, s in enumerate(stages) if s[0] is not None]
            pos = {i: k for k, i in enumerate(widx)}
            issued = 0

            def issue(upto):
                nonlocal issued
                while issued < min(upto, len(widx)):
                    src, isf, _ = stages[widx[issued]]
                    s = issued % 3
                    dst = wringf[s] if isf else wring[s]
                    dma(dst[:], src, writes=[Bw[s]])
                    issued += 1

            for i, (src, isf, fn) in enumerate(stages):
                if src is not None:
                    k = pos[i]
                    issue(k + 3)
                    s = k % 3
                    fn(wringf[s] if isf else wring[s], Bw[s])
                else:
                    fn(None, None)
            stages.clear()

        phase_alloc()
        stg32 = [alloc([128, 4096], F32) for _ in range(2)]
        stg16 = [alloc([128, 4096], BF16) for _ in range(2)]
        Bs32 = [Buf(), Buf()]
        Bs16 = [Buf(), Buf()]
        cstT = alloc([128, 128 + 1024], F32)
        Bc = Buf()
        P.op("dve", "memset", dict(ap=ones[:], constant=1.0), writes=[Bc])
        P.op("dve", "memset", dict(ap=epsT[:], constant=EPS), writes=[Bc])
        dma(cstT[:], cst.ap(), writes=[Bc])
        P.op("dve", "tensor_copy", dict(out=perm[:], in_=cstT[:, 0:128]), reads=[Bc], writes=[Bc])
        P.op("dve", "tensor_copy", dict(out=wmask[:].rearrange("p a b -> p (a b)"), in_=cstT[:, 128:1152]),
             reads=[Bc], writes=[Bc])
        dma(qknT[:], qkn.ap(), writes=[Bc])
        dma(sinkE[:], sinkb.ap(), writes=[Bc])
        P.op("act", "activation", dict(out=sinkE[:], in_=sinkE[:], func=AF.Exp), reads=[Bc], writes=[Bc])
        lam_init = 0.8 - 0.6 * math.exp(-0.3 * 2)
        lv = alloc([128, 4], F32)
        lvb = alloc([128, 2], BF16)
        lv2 = alloc([128, 2], F32)
        dma(lv[:], lvec.ap(), writes=[Bc])
        onesf = alloc([128, 128], F32)
        P.op("dve", "memset", dict(ap=onesf[:], constant=1.0), writes=[Bc])
        P.op("dve", "tensor_tensor", dict(out=lv2[:, 0:1], in0=lv[:, 0:1], in1=lv[:, 1:2], op=ALU.mult), reads=[Bc], writes=[Bc])
        P.op("dve", "tensor_tensor", dict(out=lv2[:, 1:2], in0=lv[:, 2:3], in1=lv[:, 3:4], op=ALU.mult), reads=[Bc], writes=[Bc])
        bk, bb = next_bank()
        P.op("pe", "matmul", dict(out=bk[:, 0:2], lhsT=onesf[:], rhs=lv2[:], start=True, stop=True), reads=[Bc], writes=[bb])
        P.op("act", "activation", dict(out=lamT[:, 0:2], in_=bk[:, 0:2], func=AF.Exp), reads=[bb], writes=[Bc])
        P.op("dve", "tensor_tensor", dict(out=lamT[:, 2:3], in0=lamT[:, 0:1], in1=lamT[:, 1:2], op=ALU.subtract), reads=[Bc], writes=[Bc])
        P.op("dve", "tensor_scalar", dict(out=lamT[:, 3:4], in0=lamT[:, 2:3], scalar1=-1.0, scalar2=-lam_init,
                                              op0=ALU.mult, op1=ALU.add), reads=[Bc], writes=[Bc])
        dma(sublnS[:], subln.ap(), writes=[Bc])
        P.op("dve", "tensor_scalar", dict(out=sublnS[:], in0=sublnS[:], scalar1=1.0 - lam_init, scalar2=None,
                                              op0=ALU.mult), reads=[Bc], writes=[Bc])

        cast_i = 0
        for l in range(4):
            rows = NCH[l] * 16
            r0 = 0
            lastst = []
            while r0 < rows:
                nr = min(128, rows - r0)
                for chf in range(2):
                    cs_ = slice(chf * 4096, (chf + 1) * 4096)
                    s = cast_i % 2
                    cast_i += 1
                    dma(stg32[s][0:nr, :], wl[l].ap()[r0:r0 + nr, cs_], writes=[Bs32[s]])
                    eng = "dve" if cast_i % 2 == 0 else "pool"
                    P.op(eng, "tensor_copy", dict(out=stg16[s][0:nr, :], in_=stg32[s][0:nr, :]),
                         reads=[Bs32[s]], writes=[Bs16[s]])
                    lastst.append(dma(wloc[l].ap()[r0:r0 + nr, cs_], stg16[s][0:nr, :], reads=[Bs16[s]]))
                r0 += nr
            bst = Buf()
            for o in lastst:
                o2 = o
            P.op("pool", "collective_compute", dict(
                kind="AllGather", op=ALU.bypass, replica_groups=[list(range(8))],
                ins=[wloc[l].ap().opt()], outs=[wfull[l].ap().opt()]), kind="cc")
            P.q["pool"][-1][0].deps = set(P.q["pool"][-1][0].deps) | set(lastst)
            for o in lastst:
                o.hasdep = True

        natT = alloc([128, 640], F32)
        natM = alloc([128, 640], F32)
        natE = alloc([128, 640], BF16)
        Bn = Buf()
        Bm = Buf()
        Be = Buf()
        est = []
        for hh in range(2):
            for pat in range(5):
                r = (hh * 5 + pat) * 128
                dma(natT[:], natb.ap()[r:r + 128, :], writes=[Bn])
                dma(natM[:], natm.ap()[pat * 128:(pat + 1) * 128, :], writes=[Bm])
                P.op("act", "activation", dict(out=natT[:], in_=natT[:], func=AF.Exp), reads=[Bn], writes=[Bn])
                P.op("dve", "tensor_tensor", dict(out=natE[:], in0=natT[:], in1=natM[:], op=ALU.mult),
                     reads=[Bn, Bm], writes=[Be])
                est.append(dma(eloc.ap()[r:r + 128, :], natE[:], reads=[Be]))
        P.op("pool", "collective_compute", dict(
            kind="AllGather", op=ALU.bypass, replica_groups=[list(range(8))],
            ins=[eloc.ap().opt()], outs=[etab.ap().opt()]), kind="cc")
        P.q["pool"][-1][0].deps = set(P.q["pool"][-1][0].deps) | set(est)
        for o in est:
            o.hasdep = True

        scT = alloc([128, 32, 9], F32)
        badaT = alloc([128, 144], F32)
        modL = alloc([128, 4, 36, 9], F32)
        Bsc = Buf()
        Bml = Buf()
        dma(scT[:].rearrange("p a b -> p (a b)"), ccT.ap(), writes=[Bsc])
        dma(badaT[:], bada.ap(), writes=[Bsc])
        P.op("act", "activation", dict(out=scT[:], in_=scT[:], func=AF.Silu), reads=[Bsc], writes=[Bsc])
        for l in range(4):
            for m in range(36):
                def fn(w, wb, l=l, m=m):
                    w3 = w[:].rearrange("p (k n) -> p k n", k=32)
                    bk, bb = next_bank()
                    for kc in range(32):
                        P.op("pe", "matmul", dict(out=bk[:, 0:9], lhsT=w3[:, kc, :], rhs=scT[:, kc, :],
                                                             start=(kc == 0), stop=(kc == 31)),
                             reads=[wb, Bsc], writes=[bb])
                    P.op("dve", "tensor_scalar", dict(out=modL[:, l, m, :], in0=bk[:, 0:9],
                                                          scalar1=badaT[:, l * 36 + m:l * 36 + m + 1], scalar2=None,
                                                          op0=ALU.add), reads=[bb, Bsc], writes=[Bml])
                r = (l * 36 + m) * 128
                stages.append((wada.ap()[r:r + 128, :], True, fn))
        run_stages()
        mst = dma(modloc.ap(), modL[:].rearrange("p a b c -> p (a b c)"), reads=[Bml])
        P.op("pool", "collective_compute", dict(
            kind="AllGather", op=ALU.bypass, replica_groups=[list(range(8))],
            ins=[modloc.ap().opt()], outs=[modall.ap().opt()]), kind="cc")
        ccm = P.q["pool"][-1][0]
        ccm.deps = set(ccm.deps) | {mst}
        mst.hasdep = True
        P.barrier()

        phase_alloc()
        modA = alloc([128, 8, 1296], F32)
        selT = alloc([128, 8], F32)
        ngT = alloc([128, 4, 3, 32], F32)
        accS = alloc([128, 1152], F32)
        accP = alloc([128, 1152], F32)
        modS = alloc([128, 288], F32)
        Bma = Buf()
        dma(modA[:], modall.ap().rearrange("(r p) f -> p r f", p=128), writes=[Bma])
        dma(selT[:], sel.ap(), writes=[Bma])
        dma(ngT[:].rearrange("p a b c -> p (a b c)"), normg.ap(), writes=[Bma])
        mview = modA[:].rearrange("p r (x c) -> p (r x) c", c=9)
        P.op("dve", "tensor_scalar", dict(out=accS[:], in0=mview[:, :, 0], scalar1=selT[:, 0:1], scalar2=None,
                                              op0=ALU.mult), reads=[Bma], writes=[Bma])
        for row in range(1, 8):
            P.op("dve", "scalar_tensor_tensor", dict(out=accS[:], in0=mview[:, :, row],
                                                                    scalar=selT[:, row:row + 1], in1=accS[:],
                                                                    op0=ALU.mult, op1=ALU.add), reads=[Bma], writes=[Bma])
        P.op("dve", "tensor_copy", dict(out=accP[:], in_=mview[:, :, 8]), reads=[Bma], writes=[Bma])
        for g, acc in (("S", accS), ("P", accP)):
            a4 = acc[:].rearrange("p (r l m) -> p r l m", r=8, l=4)
            for l in range(4):
                P.op("dve", "tensor_copy", dict(out=modS[:].rearrange("p (r m) -> p r m", r=8),
                                                                 in_=a4[:, :, l, :]), reads=[Bma], writes=[Bma])
                m3 = modS[:].rearrange("p (j t f) -> p j t f", j=3, t=3)
                P.op("dve", "tensor_copy", dict(out=mB[g][:, l, :, :], in_=m3[:, :, 0, :]),
                     reads=[Bma], writes=[Bma])
                P.op("dve", "tensor_scalar", dict(out=mA[g][:, l, :, :], in0=m3[:, :, 1, :],
                                                                         scalar1=1.0, scalar2=None, op0=ALU.add),
                     reads=[Bma], writes=[Bma])
                P.op("dve", "tensor_tensor", dict(out=mA[g][:, l, :, :], in0=mA[g][:, l, :, :],
                                                                 in1=ngT[:, l, :, :], op=ALU.mult), reads=[Bma], writes=[Bma])
                for j in range(3):
                    P.op("dve", "tensor_scalar", dict(
                        out=mG[g][:, l, j, :], in0=m3[:, j, 2, :], scalar1=(1.0 if j == 1 else 0.5), scalar2=None,
                        op0=ALU.mult), reads=[Bma], writes=[Bma])
        P.barrier()

        phase_alloc()
        xt = alloc([128, 32, 512], F32)
        h = alloc([128, 32, 512], BF16)
        a = alloc([128, 16, 512], BF16)
        MAIN_END = off["v"]
        Bxt = [Buf() for _ in range(32)]
        Bh = [Buf() for _ in range(32)]
        Ba = [Buf() for _ in range(16)]
        Bdx = [[Buf() for _ in range(8)] for _ in range(NT)]
        a_off = MAIN_END - 16 * 1024
        qst = [view(a_off + i * 2048, [128, 2, 512], BF16) for i in range(2)]
        vst = [view(a_off + 4096 + i * 1024, [128, 512], BF16) for i in range(2)]
        nkst = [view(a_off + 6144 + i * 2048, [128, 512], F32) for i in range(2)]
        nvst = [view(a_off + 10240 + i * 2048, [128, 512], F32) for i in range(2)]
        qhb = view(a_off + 14336, [128, 512], BF16)
        Bqst = [Buf(), Buf()]
        Bvst = [Buf(), Buf()]
        Bnk = [Buf(), Buf()]
        Bnv = [Buf(), Buf()]
        Bqh = Buf()

        xTv = xT.ap().rearrange("(k p) t -> p k t", p=128)
        yTv = yT.ap().rearrange("(k p) t -> p k t", p=128)

        def grp(ti):
            return "P" if ti == 8 else "S"

        def load_norm(ti, l, j, first):
            g = grp(ti)
            c0 = ti * TW
            src = xTv if first else yTv
            for pc in range(4):
                rd = [] if first else [Bdx[ti][2 * pc], Bdx[ti][2 * pc + 1]]
                dma(xt[:, pc * 8:(pc + 1) * 8, :], src[:, pc * 8:(pc + 1) * 8, c0:c0 + TW], reads=rd,
                    writes=Bxt[pc * 8:(pc + 1) * 8])
            bk, bb = next_bank()
            for kc in range(32):
                s = kc % 2
                P.op("act", "activation", dict(out=sq[s][:], in_=xt[:, kc, :], func=AF.Square),
                     reads=[Bxt[kc]], writes=[Bsq[s]])
                P.op("pe", "matmul", dict(out=bk[:], lhsT=ones[:], rhs=sq[s][:], start=(kc == 0), stop=(kc == 31)),
                     reads=[Bsq[s]], writes=[bb])
            P.op("act", "activation", dict(out=rstd[:], in_=bk[:], func=AF.Ln, bias=epsT[:, 0:1], scale=1.0 / D),
                 reads=[bb], writes=[Brstd])
            P.op("act", "activation", dict(out=rstd[:], in_=rstd[:], func=AF.Exp, scale=-0.5), reads=[Brstd], writes=[Brstd])
            for kc in range(32):
                s = kc % 2
                P.op("dve", "scalar_tensor_tensor", dict(
                    out=tA[s][:], in0=xt[:, kc, :], scalar=mA[g][:, l, j, kc:kc + 1], in1=rstd[:], op0=ALU.mult, op1=ALU.mult),
                    reads=[Bxt[kc], Brstd], writes=[BtA[s]])
                P.op("act", "activation", dict(out=h[:, kc, :], in_=tA[s][:], func=AF.Identity,
                                                                bias=mB[g][:, l, j, kc:kc + 1], scale=1.0),
                     reads=[BtA[s]], writes=[Bh[kc]])

        def outproj_stage(ti, l, j, gq, first_tile_op=None):
            g = grp(ti)
            c0 = ti * TW

            def fn(w, wb):
                w3 = w[:].rearrange("p (k n) -> p k n", k=16)
                for jj in range(4):
                    n = 4 * gq + jj
                    bk, bb = next_bank()
                    for kc in range(16):
                        P.op("pe", "matmul", dict(out=bk[:], lhsT=w3[:, kc, jj * 128:(jj + 1) * 128], rhs=a[:, kc, :],
                                                                     start=(kc == 0), stop=(kc == 15)),
                             reads=[wb, Ba[kc]], writes=[bb])
                    P.op("dve", "scalar_tensor_tensor", dict(
                        out=xt[:, n, :], in0=bk[:], scalar=mG[g][:, l, j, n:n + 1], in1=xt[:, n, :], op0=ALU.mult, op1=ALU.add),
                        reads=[bb, Bxt[n]], writes=[Bxt[n]])
                dma(yTv[:, 4 * gq:4 * gq + 4, c0:c0 + TW], xt[:, 4 * gq:4 * gq + 4, :], reads=Bxt[4 * gq:4 * gq + 4],
                    writes=[Bdx[ti][gq]])
            return fn

        def wsrc(l, ch):
            return wfull[l].ap()[ch * 128:(ch + 1) * 128, :]

        def ffn_sublayer(l, which):
            j = 0 if which == 0 else 2
            base = 0 if which == 0 else NCH[l] - 24
            for ti in TILES:
                for m in range(16):
                    def fn(w, wb, ti=ti, m=m):
                        if m == 0:
                            load_norm(ti, l, j, first=(l == 0 and which == 0))
                        w3 = w[:].rearrange("p (k n) -> p k n", k=32)
                        bg, bgb = next_bank()
                        bu, bub = next_bank()
                        for kc in range(32):
                            P.op("pe", "matmul", dict(out=bg[:], lhsT=w3[:, kc, 0:128], rhs=h[:, kc, :],
                                                                 start=(kc == 0), stop=(kc == 31)), reads=[wb, Bh[kc]], writes=[bgb])
                        for kc in range(32):
                            P.op("pe", "matmul", dict(out=bu[:], lhsT=w3[:, kc, 128:256], rhs=h[:, kc, :],
                                                                 start=(kc == 0), stop=(kc == 31)), reads=[wb, Bh[kc]], writes=[bub])
                        s = m % 2
                        P.op("act", "activation", dict(out=tS[s][:], in_=bg[:], func=AF.Silu), reads=[bgb], writes=[BtS[s]])
                        P.op("dve", "tensor_tensor", dict(out=a[:, m, :], in0=bu[:], in1=tS[s][:], op=ALU.mult),
                             reads=[bub, BtS[s]], writes=[Ba[m]])
                    stages.append((wsrc(l, base + m), False, fn))
                for gq in range(8):
                    stages.append((wsrc(l, base + 16 + gq), False, outproj_stage(ti, l, j, gq)))

        QTv = QT.ap().rearrange("(hh p) t -> p hh t", p=128)
        KTv = KT.ap().rearrange("(hh p) t -> p hh t", p=128)
        OTv = OT.ap().rearrange("(hh p) t -> p hh t", p=128)

        def scr_col(ti):
            return ti * TW if ti < 8 else PR0

        def phaseA(l):
            nq, nkh, dv = GEO[l]
            nqc, nkc = nq // 2, nkh // 2
            nvg = dv // 512
            base = 24
            for ti in TILES:
                c0s = scr_col(ti)
                rope = (ti < 8) and (l != 3)
                for c in range(nqc + nkc):
                    isq = c < nqc
                    def fn(w, wb, ti=ti, c=c, isq=isq, c0s=c0s, rope=rope):
                        if c == 0:
                            load_norm(ti, l, 1, first=False)
                            if rope:
                                dma(cosS[:], cosT.ap()[:, ti * TW:(ti + 1) * TW], writes=[Bcs])
                                dma(sinS[:], sinT.ap()[:, ti * TW:(ti + 1) * TW], writes=[Bcs])
                        w3 = w[:].rearrange("p (k n) -> p k n", k=32)
                        hc = c if isq else c - nqc
                        gain = qknT[:, 2 * l:2 * l + 1] if isq else qknT[:, 2 * l + 1:2 * l + 2]
                        dstv = QTv if isq else KTv
                        s2 = c % 2
                        for jj in range(2):
                            head = 2 * hc + jj
                            bk, bb = next_bank()
                            for kc in range(32):
                                P.op("pe", "matmul", dict(out=bk[:], lhsT=w3[:, kc, jj * 128:(jj + 1) * 128], rhs=h[:, kc, :],
                                                                             start=(kc == 0), stop=(kc == 31)), reads=[wb, Bh[kc]], writes=[bb])
                            P.op("act", "activation", dict(out=sq[0][:], in_=bk[:], func=AF.Square), reads=[bb], writes=[Bsq[0]])
                            P.op("dve", "tensor_copy", dict(out=tA[0][:], in_=bk[:]), reads=[bb], writes=[BtA[0]])
                            b2, b2b = next_bank()
                            P.op("pe", "matmul", dict(out=b2[:], lhsT=ones[:], rhs=sq[0][:], start=True, stop=True), reads=[Bsq[0]], writes=[b2b])
                            P.op("act", "activation", dict(out=tS[0][:], in_=b2[:], func=AF.Ln, bias=epsT[:, 0:1], scale=1.0 / HD),
                                 reads=[b2b], writes=[BtS[0]])
                            P.op("act", "activation", dict(out=tS[0][:], in_=tS[0][:], func=AF.Exp, scale=-0.5), reads=[BtS[0]], writes=[BtS[0]])
                            if not rope:
                                if isq or ti != 8:
                                    P.op("dve", "scalar_tensor_tensor", dict(
                                        out=qst[s2][:, jj, :], in0=tA[0][:], scalar=gain, in1=tS[0][:], op0=ALU.mult, op1=ALU.mult),
                                        reads=[BtA[0], BtS[0]], writes=[Bqst[s2]])
                                else:
                                    s3 = jj
                                    P.op("dve", "scalar_tensor_tensor", dict(
                                        out=nkst[s3][:], in0=tA[0][:], scalar=gain, in1=tS[0][:], op0=ALU.mult, op1=ALU.mult),
                                        reads=[BtA[0], BtS[0]], writes=[Bnk[s3]])
                                    P.op("act", "activation", dict(out=qst[s2][:, jj, :], in_=nkst[s3][:], func=AF.Identity),
                                         reads=[Bnk[s3]], writes=[Bqst[s2]])
                                    dma(nk[l].ap()[head * 128:(head + 1) * 128, :], nkst[s3][:], reads=[Bnk[s3]])
                            else:
                                P.op("dve", "scalar_tensor_tensor", dict(
                                    out=qhb[:], in0=tA[0][:], scalar=gain, in1=tS[0][:], op0=ALU.mult, op1=ALU.mult),
                                    reads=[BtA[0], BtS[0]], writes=[Bqh])
                                b3, b3b = next_bank()
                                P.op("pe", "matmul", dict(out=b3[:], lhsT=perm[:], rhs=qhb[:], start=True, stop=True), reads=[Bqh], writes=[b3b])
                                P.op("dve", "tensor_tensor", dict(out=tA[1][:], in0=qhb[:], in1=cosS[:], op=ALU.mult),
                                     reads=[Bqh, Bcs], writes=[BtA[1]])
                                P.op("dve", "tensor_tensor", dict(out=tS[1][:], in0=b3[:], in1=sinS[:], op=ALU.mult),
                                     reads=[b3b, Bcs], writes=[BtS[1]])
                                P.op("dve", "tensor_tensor", dict(out=qst[s2][:, jj, :], in0=tA[1][:], in1=tS[1][:], op=ALU.add),
                                     reads=[BtA[1], BtS[1]], writes=[Bqst[s2]])
                        dma(dstv[:, 2 * hc:2 * hc + 2, c0s:c0s + TW], qst[s2][:], reads=[Bqst[s2]])
                    ch = base + c
                    stages.append((wsrc(l, ch), False, fn))
                for vg in range(nvg):
                    vb = {}
                    for half in range(2):
                        def fn(w, wb, ti=ti, vg=vg, half=half, vb=vb, c0s=c0s):
                            w3 = w[:].rearrange("p (k n) -> p k n", k=16)
                            for stt in range(4):
                                if half == 0:
                                    vb[stt] = next_bank()
                                bk, bb = vb[stt]
                                for kc in range(16):
                                    P.op("pe", "matmul", dict(out=bk[:], lhsT=h[:, half * 16 + kc, stt * 128:(stt + 1) * 128], rhs=w3[:, kc, :],
                                        start=(half == 0 and kc == 0), stop=(half == 1 and kc == 15)),
                                        reads=[wb, Bh[half * 16 + kc]], writes=[bb])
                                if half == 1:
                                    s = stt % 2
                                    P.op("act", "activation", dict(out=vst[s][:], in_=bk[:], func=AF.Identity), reads=[bb], writes=[Bvst[s]])
                                    dma(VS.ap()[c0s + stt * 128:c0s + (stt + 1) * 128, vg * 512:(vg + 1) * 512], vst[s][:], reads=[Bvst[s]])
                                    if ti == 8:
                                        P.op("dve", "tensor_copy", dict(out=nvst[s][:], in_=bk[:]), reads=[bb], writes=[Bnv[s]])
                                        dma(nv[l].ap()[stt * 128:(stt + 1) * 128, vg * 512:(vg + 1) * 512], nvst[s][:], reads=[Bnv[s]])
                        ch = base + nqc + nkc + vg * 2 + half
                        stages.append((wsrc(l, ch), False, fn))

        def phaseB(l):
            nq, nkh, dv = GEO[l]
            off["v"] = PH0
            c32 = alloc([128, 2048], F32)
            c16 = alloc([128, 2048], BF16)
            Bc32 = Buf()
            Bc16 = Buf()
            for hk in range(nkh):
                dma(c32[:, 0:NCTX], ckT[l].ap()[hk * 128:(hk + 1) * 128, :], writes=[Bc32])
                P.op("dve", "tensor_copy", dict(out=c16[:, 0:NCTX], in_=c32[:, 0:NCTX]), reads=[Bc32], writes=[Bc16])
                dma(KT.ap()[hk * 128:(hk + 1) * 128, NS:NS + NCTX], c16[:, 0:NCTX], reads=[Bc16])
            for cch in range(2):
                dma(c32[:, 0:dv], cv[l].ap()[cch * 128:(cch + 1) * 128, :], writes=[Bc32])
                P.op("dve", "tensor_copy", dict(out=c16[:, 0:dv], in_=c32[:, 0:dv]), reads=[Bc32], writes=[Bc16])
                dma(VS.ap()[NS + cch * 128:NS + (cch + 1) * 128, 0:dv], c16[:, 0:dv], reads=[Bc16])
            P.barrier()
            off["v"] = PH0
            PTW = 256 if l == 3 else 512
            PT = [alloc([128, PTW], BF16) for _ in range(3)]
            Bpt = [Buf() for _ in range(3)]
            rl = [alloc([128, PTW], F32) for _ in range(2)]
            Brl = [Buf(), Buf()]
            cnt = {"pt": 0, "acc": 0, "rl": 0}
            sbank = [(banks[0], bankb[0]), (banks[1], bankb[1])]
            accsets = [[(banks[2], bankb[2]), (banks[3], bankb[3]), (banks[4], bankb[4])],
                       [(banks[5], bankb[5]), (banks[6], bankb[6]), (banks[7], bankb[7])]]
            scnt = {"s": 0}

            def core(q_ap, qbuf, N, chunks, n_v, fin):
                acc = accsets[cnt["acc"] % 2]
                cnt["acc"] += 1
                lb, lbb = acc[2]
                nchk = len(chunks)

                def emitS(ci):
                    kT, kc, vaps, kvb, mask = chunks[ci]
                    sb, sbb = sbank[scnt["s"] % 2]
                    scnt["s"] += 1
                    so = sb[0:kc, 0:N]
                    if len(q_ap.shape) == 3:
                        so = so.rearrange("p (g q) -> p g q", g=q_ap.shape[1])
                    P.op("pe", "matmul", dict(out=so, lhsT=kT, rhs=q_ap, start=True, stop=True),
                         reads=list(kvb) + [qbuf], writes=[sbb])
                    return sb, sbb
                nxt = emitS(0)
                for ci in range(nchk):
                    kT, kc, vaps, kvb, mask = chunks[ci]
                    sb, sbb = nxt
                    if ci + 1 < nchk:
                        nxt = emitS(ci + 1)
                    pi = cnt["pt"] % 3
                    cnt["pt"] += 1
                    pt = PT[pi]
                    P.op("act", "activation", dict(out=pt[0:kc, 0:N], in_=sb[0:kc, 0:N], func=AF.Exp, scale=SCALE),
                         reads=[sbb], writes=[Bpt[pi]])
                    if mask is not None:
                        P.op("dve", "tensor_tensor", dict(out=pt[0:kc, 0:N], in0=pt[0:kc, 0:N], in1=mask, op=ALU.mult),
                             reads=[Bpt[pi]], writes=[Bpt[pi]])
                    for v in range(n_v):
                        ob, obb = acc[v]
                        P.op("pe", "matmul", dict(out=ob[:, 0:N], lhsT=vaps[v], rhs=pt[0:kc, 0:N], start=(ci == 0), stop=(ci == nchk - 1)),
                            reads=list(kvb) + [Bpt[pi]], writes=[obb])
                    P.op("pe", "matmul", dict(out=lb[:, 0:N], lhsT=ones[0:kc, :], rhs=pt[0:kc, 0:N],
                                                                         start=(ci == 0), stop=(ci == nchk - 1)),
                         reads=[Bpt[pi]], writes=[lbb])
                fin(acc, N)

            def recip(acc, N, sink_heads=None):
                lb, lbb = acc[2]
                ri = cnt["rl"] % 2
                cnt["rl"] += 1
                r = rl[ri]
                if sink_heads is None:
                    P.op("dve", "reciprocal", dict(out=r[:, 0:N], in_=lb[:, 0:N]), reads=[lbb], writes=[Brl[ri]])
                else:
                    for gi, hd_ in enumerate(sink_heads):
                        P.op("dve", "tensor_scalar", dict(
                            out=r[:, gi * 128:(gi + 1) * 128], in0=lb[:, gi * 128:(gi + 1) * 128], scalar1=sinkE[:, hd_:hd_ + 1],
                            scalar2=None, op0=ALU.add), reads=[lbb], writes=[Brl[ri]])
                    P.op("dve", "reciprocal", dict(out=r[:, 0:N], in_=r[:, 0:N]), reads=[Brl[ri]], writes=[Brl[ri]])
                return r, Brl[ri]

            if l in (0, 1):
                NKEY = NS + NCTX
                Kb = [alloc([128, NKEY], BF16) for _ in range(2)]
                Vb = [alloc([128, NKEY // 128, 128], BF16) for _ in range(2)]
                Kp = alloc([128, 4, NPR], BF16)
                Vp = alloc([128, 4, 512], BF16)
                qb = [alloc([128, 4, 512], BF16) for _ in range(2)]
                ob_ = [alloc([128, 4, 512], BF16) for _ in range(2)]
                BK = [Buf(), Buf()]
                BV = [Buf(), Buf()]
                BKp = Buf()
                Bq = [Buf(), Buf()]
                Bo = [Buf(), Buf()]
                qi = 0
                VSv = VS.ap().rearrange("(c p) d -> p c d", p=128)
                for hk in range(4):
                    kb = hk % 2
                    dma(Kb[kb][:], KT.ap()[hk * 128:(hk + 1) * 128, 0:NKEY], writes=[BK[kb]])
                    for c8 in range(0, NKEY // 128, 8):
                        ce = min(c8 + 8, NKEY // 128)
                        dma(Vb[kb][:, c8:ce, :], VSv[:, c8:ce, hk * 128:(hk + 1) * 128], writes=[BV[kb]])
                    if hk == 0:
                        dma(Kp[:], KTv[:, 0:4, PR0:PR0 + NPR], writes=[BKp])
                        dma(Vp[:], VSv[:, PR0 // 128:PR0 // 128 + 4, 0:512], writes=[BKp])
                    for t in range(NT):
                        c0s = scr_col(t)
                        qs = qi % 2
                        qi += 1
                        dma(qb[qs][:], QTv[:, hk * 4:hk * 4 + 4, c0s:c0s + TW], writes=[Bq[qs]])
                        for qblk in range(4):
                            q_ap = qb[qs][:, :, qblk * 128:(qblk + 1) * 128]
                            chunks = []
                            if t < 8:
                                jg = t * 4 + qblk
                                if l == 0:
                                    clist = [(c, None) for c in range(NKEY // 128)]
                                else:
                                    clist = []
                                    if jg > 0:
                                        clist.append((jg - 1, wmask[:, 0, :]))
                                    clist.append((jg, None))
                                    if jg < 31:
                                        clist.append((jg + 1, wmask[:, 1, :]))
                                    clist += [(32, None), (33, None)]
                                for c, mk_ in clist:
                                    chunks.append((Kb[kb][:, c * 128:(c + 1) * 128], 128, [Vb[kb][:, c, :]], (BK[kb], BV[kb]), mk_))
                            else:
                                sq_ = qblk // 2
                                for c in range(2):
                                    cc = sq_ * 2 + c
                                    chunks.append((Kp[:, hk, cc * 128:(cc + 1) * 128], 128, [Vp[:, cc, hk * 128:(hk + 1) * 128]], (BKp,), None))

                            def fin(acc, N, qs=qs, qblk=qblk, hk=hk):
                                r, rb = recip(acc, N, sink_heads=[hk * 4 + g for g in range(4)] if l == 1 else None)
                                ob, obb = acc[0]
                                P.op("dve", "tensor_tensor", dict(
                                    out=ob_[qs][:, :, qblk * 128:(qblk + 1) * 128],
                                    in0=ob[:].rearrange("p (g q) -> p g q", g=4), in1=r[:].rearrange("p (g q) -> p g q", g=4), op=ALU.mult),
                                    reads=[obb, rb], writes=[Bo[qs]])
                            chunks2 = []
                            for (kT, kc, vaps, kvb, mk_) in chunks:
                                chunks2.append((kT, kc, vaps, kvb, mk_))
                            core3(q_ap, Bq[qs], chunks2, fin, core, PT)
                        dma(OTv[:, hk * 4:hk * 4 + 4, c0s:c0s + TW], ob_[qs][:], reads=[Bo[qs]])
            elif l == 2:
                NKEY = NS + NCTX
                Kb = [alloc([128, 2, NKEY], BF16) for _ in range(1)]
                Vb = [alloc([128, NKEY // 128, 256], BF16) for _ in range(1)]
                Kp = alloc([128, 4, NPR], BF16)
                Vp = alloc([128, 4, 512], BF16)
                qb = [alloc([128, 4, 512], BF16) for _ in range(2)]
                ob_ = [alloc([128, 8, 512], BF16) for _ in range(2)]
                o0 = alloc([128, 2, 512], F32)
                dd = alloc([128, 2, 512], F32)
                rs2 = alloc([128, 512], F32)
                BK = [Buf()]
                BV = [Buf()]
                BKp = Buf()
                Bq = [Buf(), Buf()]
                Bo = [Buf(), Buf()]
                Bo0 = Buf()
                Bdd = Buf()
                Brs2 = Buf()
                VSv = VS.ap().rearrange("(c p) d -> p c d", p=128)
                QT4 = QT.ap().rearrange("(hg i p) t -> p i hg t", p=128, i=2)
                OT4 = OT.ap().rearrange("(hg v p) t -> p hg v t", p=128, v=2)
                qi = 0
                oi = 0
                for hk in range(2):
                    dma(Kb[0][:], KTv[:, hk * 2:hk * 2 + 2, 0:NKEY], writes=[BK[0]])
                    for c8 in range(0, NKEY // 128, 8):
                        ce = min(c8 + 8, NKEY // 128)
                        dma(Vb[0][:, c8:ce, :], VSv[:, c8:ce, hk * 256:(hk + 1) * 256], writes=[BV[0]])
                    if hk == 0:
                        dma(Kp[:], KTv[:, 0:4, PR0:PR0 + NPR], writes=[BKp])
                        dma(Vp[:], VSv[:, PR0 // 128:PR0 // 128 + 4, 0:512], writes=[BKp])
                    for t in range(NT):
                        c0s = scr_col(t)
                        os_ = oi % 2
                        oi += 1
                        qslots = []
                        for i in range(2):
                            qs = qi % 2
                            qi += 1
                            dma(qb[qs][:], QT4[:, i, hk * 4:hk * 4 + 4, c0s:c0s + TW], writes=[Bq[qs]])
                            qslots.append(qs)
                        for qblk in range(4):
                            for i in range(2):
                                qs = qslots[i]
                                q_ap = qb[qs][:, :, qblk * 128:(qblk + 1) * 128]
                                chunks = []
                                if t < 8:
                                    for c in range(NKEY // 128):
                                        chunks.append((Kb[0][:, i, c * 128:(c + 1) * 128], 128,
                                                       [Vb[0][:, c, 0:128], Vb[0][:, c, 128:256]], (BK[0], BV[0]), None))
                                else:
                                    sq_ = qblk // 2
                                    for c in range(2):
                                        cc = sq_ * 2 + c
                                        chunks.append((Kp[:, hk * 2 + i, cc * 128:(cc + 1) * 128], 128,
                                                       [Vp[:, cc, hk * 256:hk * 256 + 128], Vp[:, cc, hk * 256 + 128:hk * 256 + 256]], (BKp,), None))

                                def fin(acc, N, i=i, qblk=qblk, os_=os_):
                                    r, rb = recip(acc, N)
                                    if i == 0:
                                        for v in range(2):
                                            ob, obb = acc[v]
                                            P.op("dve", "tensor_tensor", dict(out=o0[:, v, :], in0=ob[:], in1=r[:], op=ALU.mult),
                                                 reads=[obb, rb], writes=[Bo0])
                                    else:
                                        for v in range(2):
                                            ob, obb = acc[v]
                                            P.op("dve", "tensor_tensor", dict(out=dd[:, v, :], in0=ob[:], in1=r[:], op=ALU.mult),
                                                 reads=[obb, rb], writes=[Bdd])
                                            P.op("dve", "scalar_tensor_tensor", dict(
                                                out=dd[:, v, :], in0=dd[:, v, :], scalar=lamT[:, 3:4], in1=o0[:, v, :], op0=ALU.mult, op1=ALU.add),
                                                reads=[Bdd, Bo0], writes=[Bdd])
                                        sbk, sbkb = sbank[scnt["s"] % 2]
                                        scnt["s"] += 1
                                        for v in range(2):
                                            P.op("act", "activation", dict(out=sq[v][:], in_=dd[:, v, :], func=AF.Square), reads=[Bdd], writes=[Bsq[v]])
                                            P.op("pe", "matmul", dict(out=sbk[:], lhsT=ones[:], rhs=sq[v][:], start=(v == 0), stop=(v == 1)),
                                                 reads=[Bsq[v]], writes=[sbkb])
                                        P.op("act", "activation", dict(out=rs2[:], in_=sbk[:], func=AF.Ln, bias=epsT[:, 0:1], scale=1.0 / 256),
                                             reads=[sbkb], writes=[Brs2])
                                        P.op("act", "activation", dict(out=rs2[:], in_=rs2[:], func=AF.Exp, scale=-0.5), reads=[Brs2], writes=[Brs2])
                                        for v in range(2):
                                            dst = ob_[os_][:].rearrange("p (g v) t -> p g v t", v=2)[:, :, v, qblk * 128:(qblk + 1) * 128]
                                            P.op("dve", "scalar_tensor_tensor", dict(
                                                out=dst, in0=dd[:, v, :].rearrange("p (g q) -> p g q", g=4), scalar=sublnS[:, v:v + 1],
                                                in1=rs2[:].rearrange("p (g q) -> p g q", g=4), op0=ALU.mult, op1=ALU.mult),
                                                reads=[Bdd, Brs2], writes=[Bo[os_]])
                                core3(q_ap, Bq[qs], chunks, fin, core, PT, n_v=2)
                        dma(OTv[:, hk * 8:hk * 8 + 8, c0s:c0s + TW], ob_[os_][:], reads=[Bo[os_]])
            else:
                Kw = [alloc([128, 8, 640], BF16) for _ in range(2)]
                Vw = [alloc([128, 5, 1024], BF16) for _ in range(2)]
                Kc = alloc([128, 8, NCTX], BF16)
                Vc = alloc([128, 2, 1024], BF16)
                Ei = alloc([128, 8, 640], BF16)
                Eb = alloc([128, 8, 640], BF16)
                qb = [alloc([128, 8, 128], BF16) for _ in range(2)]
                ob_ = [alloc([128, 8, 128], BF16) for _ in range(2)]
                Kp = alloc([128, 8, NPR], BF16)
                Vp = alloc([128, 4, 1024], BF16)
                qp = alloc([128, 8, NPR], BF16)
                op_ = alloc([128, 8, NPR], BF16)
                BKw = [Buf(), Buf()]
                BVw = [Buf(), Buf()]
                BKc = Buf()
                BEi = Buf()
                BEb = Buf()
                Bq = [Buf(), Buf()]
                Bo = [Buf(), Buf()]
                BKp = Buf()
                Bqp = Buf()
                Bop = Buf()
                VSv = VS.ap().rearrange("(c p) d -> p c d", p=128)
                ETv = etab.ap().rearrange("(hh pat p) f -> p hh pat f", p=128, pat=5)
                for half in range(2):
                    h0 = half * 8
                    dma(Kc[:], KTv[:, h0:h0 + 8, NS:NS + NCTX], writes=[BKc])
                    dma(Vc[:], VSv[:, NS // 128:NS // 128 + 2, half * 1024:(half + 1) * 1024], writes=[BKc])
                    dma(Ei[:], ETv[:, h0:h0 + 8, 2, :], writes=[BEi])
                    for jq in range(32):
                        s0 = min(max(2 * jq - 4, 0), 54)
                        pat = {0: 0, 1: 1, 30: 3, 31: 4}.get(jq, 2)
                        ws = jq % 2
                        dma(Kw[ws][:], KTv[:, h0:h0 + 8, s0 * 64:s0 * 64 + 640], writes=[BKw[ws]])
                        dma(Vw[ws][:], VS.ap()[s0 * 64:s0 * 64 + 640, half * 1024:(half + 1) * 1024].rearrange("(c p) d -> p c d", p=128),
                            writes=[BVw[ws]])
                        dma(qb[ws][:], QTv[:, h0:h0 + 8, jq * 128:(jq + 1) * 128], writes=[Bq[ws]])
                        if pat != 2:
                            dma(Eb[:], ETv[:, h0:h0 + 8, pat, :], writes=[BEb])
                        E = Ei if pat == 2 else Eb
                        BE = BEi if pat == 2 else BEb
                        for hh in range(8):
                            chunks = []
                            for c in range(5):
                                chunks.append((Kw[ws][:, hh, c * 128:(c + 1) * 128], 128, [Vw[ws][:, c, hh * 128:(hh + 1) * 128]],
                                               (BKw[ws], BVw[ws], BE), E[:, hh, c * 128:(c + 1) * 128]))
                            for c in range(2):
                                chunks.append((Kc[:, hh, c * 128:(c + 1) * 128], 128, [Vc[:, c, hh * 128:(hh + 1) * 128]], (BKc,), None))

                            def fin(acc, N, ws=ws, hh=hh):
                                r, rb = recip(acc, N)
                                ob, obb = acc[0]
                                P.op("dve", "tensor_tensor", dict(out=ob_[ws][:, hh, :], in0=ob[:, 0:128], in1=r[:, 0:128], op=ALU.mult),
                                     reads=[obb, rb], writes=[Bo[ws]])
                            core(qb[ws][:, hh, :], Bq[ws], 128, chunks, 1, fin)
                        dma(OTv[:, h0:h0 + 8, jq * 128:(jq + 1) * 128], ob_[ws][:], reads=[Bo[ws]])
                    dma(Kp[:], KTv[:, h0:h0 + 8, PR0:PR0 + NPR], writes=[BKp])
                    dma(Vp[:], VSv[:, PR0 // 128:PR0 // 128 + 4, half * 1024:(half + 1) * 1024], writes=[BKp])
                    dma(qp[:], QTv[:, h0:h0 + 8, PR0:PR0 + NPR], writes=[Bqp])
                    for sq_ in range(2):
                        for hh in range(8):
                            chunks = []
                            for c in range(2):
                                cc = sq_ * 2 + c
                                chunks.append((Kp[:, hh, cc * 128:(cc + 1) * 128], 128, [Vp[:, cc, hh * 128:(hh + 1) * 128]], (BKp,), None))

                            def fin(acc, N, sq_=sq_, hh=hh):
                                r, rb = recip(acc, N)
                                ob, obb = acc[0]
                                P.op("dve", "tensor_tensor", dict(out=op_[:, hh, sq_ * 256:(sq_ + 1) * 256], in0=ob[:, 0:256], in1=r[:, 0:256], op=ALU.mult),
                                     reads=[obb, rb], writes=[Bop])
                            core(qp[:, hh, sq_ * 256:(sq_ + 1) * 256], Bqp, 256, chunks, 1, fin)
                    dma(OTv[:, h0:h0 + 8, PR0:PR0 + NPR], op_[:], reads=[Bop])
            P.barrier()

        def core3(q_ap, qbuf, chunks, fin, core, PT, n_v=1):
            core(q_ap, qbuf, 512, chunks, n_v, fin)

        def phaseC(l):
            base = 24 + (12 if l < 3 else 24)
            for ti in TILES:
                c0 = ti * TW
                c0s = scr_col(ti)
                for gq in range(8):
                    def fn(w, wb, ti=ti, gq=gq, c0=c0, c0s=c0s):
                        if gq == 0:
                            dma(a[:], OTv[:, 0:16, c0s:c0s + TW], writes=Ba)
                            for pc in range(4):
                                dma(xt[:, pc * 8:(pc + 1) * 8, :], yTv[:, pc * 8:(pc + 1) * 8, c0:c0 + TW],
                                    reads=[Bdx[ti][2 * pc], Bdx[ti][2 * pc + 1]], writes=Bxt[pc * 8:(pc + 1) * 8])
                        outproj_stage(ti, l, 1, gq)(w, wb)
                    stages.append((wsrc(l, base + gq), False, fn))

        def main_seq():
            for l in range(4):
                ffn_sublayer(l, 0)
                if stop == f"ffn{l}_0":
                    run_stages()
                    return
                phaseA(l)
                run_stages()
                P.barrier()
                if stop == f"A{l}":
                    return
                phaseB(l)
                off["v"] = MAIN_END
                if stop == f"B{l}":
                    return
                phaseC(l)
                if stop == f"C{l}":
                    run_stages()
                    return
                ffn_sublayer(l, 1)
                run_stages()
                P.barrier()
                if stop == f"ffn{l}_1":
                    return
        main_seq()
        print("ops per engine:", {e: len(P.q[e]) for e in P.ENGS})
        P.emit(st)
    return nc


def _tile_fm(W, ncols_per_chunk):
    K, N = W.shape
    kc = K // 128
    nch = N // ncols_per_chunk
    return np.ascontiguousarray(W.reshape(kc, 128, nch, ncols_per_chunk).transpose(2, 1, 0, 3)).reshape(nch * 128, kc * ncols_per_chunk)


def _layer_weights(l, w_ffn_in, w_ffn_out, wqkv, wo):
    parts = []

    def ffn(which):
        W = w_ffn_in[l, which]
        g = W[:, :2048].reshape(32, 128, 16, 128).transpose(2, 1, 0, 3)
        u = W[:, 2048:].reshape(32, 128, 16, 128).transpose(2, 1, 0, 3)
        parts.append(np.concatenate([g, u], axis=3).reshape(16 * 128, 8192))
        parts.append(_tile_fm(w_ffn_out[l, which], 512))
    ffn(0)
    nq, nkh, dv = GEO[l]
    qk = wqkv[:, :(nq + nkh) * 128]
    parts.append(_tile_fm(qk, 256))
    V = wqkv[:, (nq + nkh) * 128:]
    for vg in range(dv // 512):
        Vg = V[:, vg * 512:(vg + 1) * 512]
        parts.append(np.ascontiguousarray(Vg.reshape(2, 16, 128, 512).transpose(0, 2, 1, 3)).reshape(2 * 128, 8192))
    parts.append(_tile_fm(wo, 512))
    ffn(1)
    out = np.concatenate(parts, axis=0)
    assert out.shape == (NCH[l] * 128, 8192), out.shape
    return out


def _nat_tables(rpb):
    kk = np.arange(640)
    qq = np.arange(128)
    kr_rel = kk // 64
    kcol = kk % 64
    r_rel = qq // 64
    cl = qq % 64
    pats = [(0, 0), (0, 2), (0, 4), (54, 60), (54, 62)]
    B = np.zeros((16, 5, 640, 128), np.float32)
    M = np.zeros((5, 640, 128), np.float32)
    for pi, (s, r0) in enumerate(pats):
        kr = (s + kr_rel)[:, None]
        r = (r0 + r_rel)[None, :]
        rs = np.clip(r - 4, 0, 56)
        cs = np.clip(cl[None, :] - 8, 0, 48)
        kc = kcol[:, None]
        ok = (kr >= rs) & (kr < rs + 8) & (kc >= cs) & (kc < cs + 16)
        dr = np.clip(kr - r + 7, 0, 14)
        dc = np.clip(kc - cl[None, :] + 15, 0, 30)
        dr_b = np.broadcast_to(dr, (640, 128))
        dc_b = np.broadcast_to(dc, (640, 128))
        B[:, pi] = rpb[:, dr_b, dc_b]
        M[pi] = ok
    B = B.reshape(16, 5, 5, 128, 128).transpose(0, 1, 3, 2, 4).reshape(16, 5 * 128, 640)
    M = M.reshape(5, 5, 128, 128).transpose(0, 2, 1, 3).reshape(5 * 128, 640)
    return np.ascontiguousarray(B), np.ascontiguousarray(M)


_NC_CACHE = {}


def _prep(x_prompt, x_sample, cache_k0, cache_v0, cache_k1, cache_v1, cache_k2, cache_v2,
           cache_k3, cache_v3, c, c_ctx, norm_g, w_ada, b_ada, w_ffn_in, w_ffn_out,
           att_wqkv, att_wo, att_qn, att_kn,
           win_wqkv, win_wo, win_qn, win_kn, win_sink,
           diff_wqkv, diff_wo, diff_qn, diff_kn, diff_lq1, diff_lk1, diff_lq2, diff_lk2, diff_subln,
           nat_wqkv, nat_wo, nat_qn, nat_kn, nat_rpb):
    f = lambda v: np.asarray(v, dtype=np.float32)
    x_prompt, x_sample = f(x_prompt), f(x_sample)
    w_ffn_in, w_ffn_out, w_ada, b_ada = f(w_ffn_in), f(w_ffn_out), f(w_ada), f(b_ada)
    wqkvs = [f(att_wqkv), f(win_wqkv), f(diff_wqkv), f(nat_wqkv)]
    wos = [f(att_wo), f(win_wo), f(diff_wo), f(nat_wo)]
    cks = [f(cache_k0), f(cache_k1), f(cache_k2), f(cache_k3)]
    cvs = [f(cache_v0), f(cache_v1), f(cache_v2), f(cache_v3)]
    rep = lambda v: np.ascontiguousarray(np.broadcast_to(f(v).reshape(1, -1), (128, f(v).size)))

    wtiles = [_layer_weights(l, w_ffn_in, w_ffn_out, wqkvs[l], wos[l]) for l in range(4)]
    natB, natM = _nat_tables(f(nat_rpb))
    t = np.arange(NS)
    inv = (10000.0 ** (-np.arange(32, dtype=np.float32) / 32)).astype(np.float32)
    ang = np.concatenate([(t // 64).astype(np.float32)[:, None] * inv, (t % 64).astype(np.float32)[:, None] * inv], axis=-1)
    cosT = np.concatenate([np.cos(ang).T, np.cos(ang).T], axis=0).astype(np.float32)
    sinT = np.concatenate([-np.sin(ang).T, np.sin(ang).T], axis=0).astype(np.float32)
    perm = np.zeros((128, 128), np.float32)
    perm[np.arange(128), (np.arange(128) + 64) % 128] = 1.0
    aa = np.arange(128)[:, None]
    bb = np.arange(128)[None, :]
    mprev = np.tile((bb <= aa).astype(np.float32), (1, 4))
    mnext = np.tile((aa <= bb).astype(np.float32), (1, 4))
    cst = np.concatenate([perm, mprev, mnext], axis=1)
    qkn = np.stack([f(att_qn), f(att_kn), f(win_qn), f(win_kn), f(diff_qn), f(diff_kn), f(nat_qn), f(nat_kn)], axis=1)
    lvec = np.stack([f(diff_lq1), f(diff_lk1), f(diff_lq2), f(diff_lk2)], axis=1)
    subln = f(diff_subln).reshape(2, 128).T
    normg = f(norm_g).reshape(4, 3, 32, 128).transpose(3, 0, 1, 2).reshape(128, 384)
    cc = np.concatenate([f(c), f(c_ctx)[None, :]], axis=0)
    ccT = cc.reshape(9, 32, 128).transpose(2, 1, 0).reshape(128, 288)
    wa = w_ada.reshape(4, 32, 128, 8, 36, 128)

    in_maps = []
    for core in range(8):
        xs = x_sample[core]
        xp = x_prompt[2 * core:2 * core + 2].reshape(NPR, D)
        xTc = np.ascontiguousarray(np.concatenate([xs, xp], axis=0).T)
        m = {"xT": xTc}
        for l in range(4):
            rows = NCH[l] * 16
            m[f"wl{l}"] = wtiles[l][core * rows:(core + 1) * rows]
        m["wada"] = np.ascontiguousarray(wa[:, :, :, core].transpose(0, 3, 2, 1, 4)).reshape(4 * 36 * 128, 4096)
        m["bada"] = np.ascontiguousarray(b_ada.reshape(4, 8, 36, 128)[:, core].transpose(2, 0, 1)).reshape(128, 144)
        m["ccT"] = np.ascontiguousarray(ccT)
        s = np.zeros((128, 8), np.float32)
        s[:, core] = 1.0
        m["sel"] = s
        m["normg"] = np.ascontiguousarray(normg)
        m["qkn"] = np.ascontiguousarray(qkn)
        m["sinkb"] = rep(win_sink)
        m["lvec"] = np.ascontiguousarray(lvec)
        m["subln"] = np.ascontiguousarray(subln)
        m["cosT"] = cosT
        m["sinT"] = sinT
        m["cst"] = cst
        for l in range(4):
            nkh, dv = GEO[l][1], GEO[l][2]
            ck = cks[l][core].reshape(NCTX, nkh, 128)
            m[f"ckT{l}"] = np.ascontiguousarray(ck.transpose(1, 2, 0)).reshape(nkh * 128, NCTX)
            m[f"cv{l}"] = np.ascontiguousarray(cvs[l][core].reshape(NCTX, dv))
        m["natb"] = np.ascontiguousarray(natB[2 * core:2 * core + 2]).reshape(2 * 5 * 128, 640)
        m["natm"] = natM
        in_maps.append(m)

    return in_maps


def _assemble(R):
    y_prompt = np.empty((16, 256, D), np.float32)
    y_sample = np.empty((8, NS, D), np.float32)
    news = []
    for l in range(4):
        nkh, dv = GEO[l][1], GEO[l][2]
        news.append((np.empty((16, 256, nkh, 128), np.float32), np.empty((16, 256, dv), np.float32)))
    for core in range(8):
        yTc = R[core]["yT"]
        y_sample[core] = yTc[:, :NS].T
        y_prompt[2 * core:2 * core + 2] = yTc[:, NS:].T.reshape(2, 256, D)
        for l in range(4):
            nkh, dv = GEO[l][1], GEO[l][2]
            k = R[core][f"nk{l}"].reshape(nkh, 128, 2, 256).transpose(2, 3, 0, 1)
            news[l][0][2 * core:2 * core + 2] = k
            news[l][1][2 * core:2 * core + 2] = R[core][f"nv{l}"].reshape(2, 256, dv)
    outs = [y_prompt, y_sample]
    shapes_k = [(16, 256, 4, 128), (16, 256, 4, 128), (16, 256, 2, 2, 128), (16, 256, 16, 128)]
    shapes_v = [(16, 256, 4, 128), (16, 256, 4, 128), (16, 256, 2, 256), (16, 256, 16, 128)]
    for l in range(4):
        outs.append(news[l][0].reshape(shapes_k[l]))
        outs.append(news[l][1].reshape(shapes_v[l]))
    return tuple(outs)


def kernel(**inputs):
    in_maps = _prep(**inputs)
    if "nc" not in _NC_CACHE:
        _NC_CACHE["nc"] = build_nc()
    res = run_bass_kernel_spmd(_NC_CACHE["nc"], in_maps, core_ids=list(range(8)))
    return _assemble(res.results)
```
